# Optimizing a Trainium2 kernel written in Bass

```python
import math
import jax, jax.numpy as jnp
from jax import lax
import numpy as np

D_MODEL = 1024
BATCH = 2
SEQ = 8192
DEPTH = 2

N_META = 16
N_A = DEPTH // 2
N_B = DEPTH - N_A
N_HEADS = 8
HEAD_DIM = D_MODEL // N_HEADS
D_FF = 2816
CONV_WIDTH = 3
BLOCK = 128
PAD = (-N_META) % BLOCK
LN_EPS = 1e-5
DEEPNORM_ALPHA = (2 * DEPTH) ** 0.25
DEEPNORM_BETA = (8 * DEPTH) ** -0.25
NEG_INF = -1e30

kernel_name = "yoco_shortconv_fox_macaron_deepnorm"


def layer_norm(x, g, b):
    xf = x.astype(jnp.float32)
    mu = jnp.mean(xf, axis=-1, keepdims=True)
    var = jnp.mean(jnp.square(xf - mu), axis=-1, keepdims=True)
    y = (xf - mu) * lax.rsqrt(var + LN_EPS) * g.astype(jnp.float32) + b.astype(jnp.float32)
    return y.astype(x.dtype)


def swiglu(x, wg, wu, wd):
    return (jax.nn.silu(x @ wg) * (x @ wu)) @ wd


def short_conv(x, w_in, w_conv, w_out):
    bgate, cgate, val = jnp.split(x @ w_in, 3, axis=-1)
    u = cgate * val
    y = lax.conv_general_dilated(
        u, w_conv[:, None, :].astype(u.dtype),
        window_strides=(1,), padding=[(CONV_WIDTH - 1, 0)],
        dimension_numbers=("NWC", "WIO", "NWC"),
        feature_group_count=D_MODEL)
    return (bgate * y) @ w_out


def shared_kv(h, kv_w, f_bias):
    bsz, L, _ = h.shape
    kvf = h @ kv_w
    k = kvf[..., :D_MODEL].reshape(bsz, L, N_HEADS, HEAD_DIM)
    v = kvf[..., D_MODEL:2 * D_MODEL].reshape(bsz, L, N_HEADS, HEAD_DIM)
    f_logit = (kvf[..., 2 * D_MODEL:] + f_bias).astype(jnp.float32)
    log_f = jax.nn.log_sigmoid(f_logit)
    pad4 = ((0, 0), (PAD, 0), (0, 0), (0, 0))
    k = jnp.pad(k, pad4).transpose(0, 2, 1, 3)
    v = jnp.pad(v, pad4).transpose(0, 2, 1, 3)
    log_f = jnp.pad(log_f, ((0, 0), (PAD, 0), (0, 0)))
    c = jnp.cumsum(log_f, axis=1).transpose(0, 2, 1)
    return k, v, c


def forgetting_attention(h, w_q, w_o, k, v, c):
    bsz, L, _ = h.shape
    Lp = L + PAD
    n_blocks = Lp // BLOCK
    scale = 1.0 / math.sqrt(HEAD_DIM)
    q = (h @ w_q).reshape(bsz, L, N_HEADS, HEAD_DIM)
    q = jnp.pad(q, ((0, 0), (PAD, 0), (0, 0), (0, 0))).transpose(0, 2, 1, 3)
    k_pos = jnp.arange(Lp)

    def one_block(i):
        start = i * BLOCK
        qb = lax.dynamic_slice_in_dim(q, start, BLOCK, axis=2)
        cq = lax.dynamic_slice_in_dim(c, start, BLOCK, axis=2)
        s = jnp.einsum("bhqd,bhkd->bhqk", qb, k).astype(jnp.float32) * scale
        s = s + cq[..., :, None] - c[..., None, :]
        q_pos = start + jnp.arange(BLOCK)
        mask = (k_pos[None, :] <= q_pos[:, None]) & (k_pos[None, :] >= PAD)
        s = jnp.where(mask, s, NEG_INF)
        p = jax.nn.softmax(s, axis=-1)
        return jnp.einsum("bhqk,bhkd->bhqd", p.astype(v.dtype), v)

    o = lax.map(one_block, jnp.arange(n_blocks))
    o = o.transpose(1, 0, 3, 2, 4).reshape(bsz, Lp, D_MODEL)[:, PAD:]
    return o @ w_o


def setup_inputs(seed: int = 0) -> dict:
    key = jax.random.key(seed)
    ks = jax.random.split(key, 20)
    f32 = jnp.float32

    def nrm(k, shape, scale):
        return jax.random.normal(k, shape, f32) * scale

    d_s = D_MODEL ** -0.5
    f_s = D_FF ** -0.5
    x = nrm(ks[0], (BATCH, SEQ, D_MODEL), 1.0)
    meta = nrm(ks[1], (N_META, D_MODEL), 1.0)
    ffn1_wg = nrm(ks[2], (DEPTH, D_MODEL, D_FF), d_s)
    ffn1_wu = nrm(ks[3], (DEPTH, D_MODEL, D_FF), d_s)
    ffn1_wd = nrm(ks[4], (DEPTH, D_FF, D_MODEL), f_s * DEEPNORM_BETA)
    ffn2_wg = nrm(ks[5], (DEPTH, D_MODEL, D_FF), d_s)
    ffn2_wu = nrm(ks[6], (DEPTH, D_MODEL, D_FF), d_s)
    ffn2_wd = nrm(ks[7], (DEPTH, D_FF, D_MODEL), f_s * DEEPNORM_BETA)
    ln_gain = 1.0 + nrm(ks[8], (DEPTH, 3, D_MODEL), 0.02)
    ln_bias = nrm(ks[9], (DEPTH, 3, D_MODEL), 0.02)
    conv_w_in = nrm(ks[10], (N_A, D_MODEL, 3 * D_MODEL), d_s)
    conv_w = nrm(ks[11], (N_A, CONV_WIDTH, D_MODEL), CONV_WIDTH ** -0.5)
    conv_w_out = nrm(ks[12], (N_A, D_MODEL, D_MODEL), d_s * DEEPNORM_BETA)
    w_k = nrm(ks[13], (D_MODEL, D_MODEL), d_s)
    w_v = nrm(ks[14], (D_MODEL, D_MODEL), d_s * DEEPNORM_BETA)
    w_f = nrm(ks[15], (D_MODEL, N_HEADS), d_s * 0.5)
    kv_w = jnp.concatenate([w_k, w_v, w_f], axis=1)
    f_bias = 3.0 + nrm(ks[16], (N_HEADS,), 0.5)
    attn_w_q = nrm(ks[17], (N_B, D_MODEL, D_MODEL), d_s)
    attn_w_o = nrm(ks[18], (N_B, D_MODEL, D_MODEL), d_s * DEEPNORM_BETA)
    return {"x": x, "meta": meta,
            "ffn1_wg": ffn1_wg, "ffn1_wu": ffn1_wu, "ffn1_wd": ffn1_wd,
            "ffn2_wg": ffn2_wg, "ffn2_wu": ffn2_wu, "ffn2_wd": ffn2_wd,
            "ln_gain": ln_gain, "ln_bias": ln_bias,
            "conv_w_in": conv_w_in, "conv_w": conv_w, "conv_w_out": conv_w_out,
            "kv_w": kv_w, "f_bias": f_bias,
            "attn_w_q": attn_w_q, "attn_w_o": attn_w_o}


def reference(x, meta, ffn1_wg, ffn1_wu, ffn1_wd, ffn2_wg, ffn2_wu, ffn2_wd,
              ln_gain, ln_bias, conv_w_in, conv_w, conv_w_out, kv_w, f_bias,
              attn_w_q, attn_w_o):
    bsz = x.shape[0]
    h = jnp.concatenate(
        [jnp.broadcast_to(meta.astype(x.dtype)[None], (bsz, N_META, D_MODEL)), x], axis=1)
    k_sh = v_sh = c_sh = None
    for l in range(DEPTH):
        h = layer_norm(DEEPNORM_ALPHA * h + 0.5 * swiglu(h, ffn1_wg[l], ffn1_wu[l], ffn1_wd[l]),
                       ln_gain[l, 0], ln_bias[l, 0])
        if l < N_A:
            mix = short_conv(h, conv_w_in[l], conv_w[l], conv_w_out[l])
        else:
            j = l - N_A
            mix = forgetting_attention(h, attn_w_q[j], attn_w_o[j], k_sh, v_sh, c_sh)
        h = layer_norm(DEEPNORM_ALPHA * h + mix, ln_gain[l, 1], ln_bias[l, 1])
        h = layer_norm(DEEPNORM_ALPHA * h + 0.5 * swiglu(h, ffn2_wg[l], ffn2_wu[l], ffn2_wd[l]),
                       ln_gain[l, 2], ln_bias[l, 2])
        if l == N_A - 1:
            k_sh, v_sh, c_sh = shared_kv(h, kv_w, f_bias)
    return h[:, N_META:]
```

```python
import numpy as np
from contextlib import ExitStack
import concourse.bass as bass
import concourse.mybir as mybir
from concourse.bass_utils import run_bass_kernel_spmd

F32 = mybir.dt.float32
BF16 = mybir.dt.bfloat16
U8 = mybir.dt.uint8
I32 = mybir.dt.int32
ALU = mybir.AluOpType
AF = mybir.ActivationFunctionType

D = 1024
DC = 8
FF = 2816
FC = 22
NH = 8
SEQ = 8192
NMETA = 16
SLOT = 512
HALO = 16
SW = SLOT + HALO
NSLOT = 4
NCOL = NSLOT * SW
NLOC = HALO + NSLOT * SLOT
NTG = 17
ALPHA = 4.0 ** 0.25
EPS_P = 1e-5 / (ALPHA * ALPHA)
K_FFN = 0.5 / ALPHA
K_MIX = 1.0 / ALPHA
QSCALE = 1.0 / np.sqrt(128.0)
NEG = -1.0e30
NW1 = 2
DELAY_T = True
NKV = 4


class RecIns:
    def __init__(self, rec):
        self.rec = rec

    def then_inc(self, sem, val=1):
        self.rec.append((sem, val))
        return self


class RecEngine:
    def __init__(self):
        self.prog = []

    def __getattr__(self, name):
        def f(*a, **k):
            rec = [name, a, k]
            self.prog.append(rec)
            return RecIns(rec)
        return f

    def replay(self, e, special=None):
        for rec in self.prog:
            if rec[0] == "special":
                special(e, *rec[1])
                continue
            a = [x() if getattr(x, "_lazy", False) else x for x in rec[1]]
            k = {kk: (x() if getattr(x, "_lazy", False) else x) for kk, x in rec[2].items()}
            ins = getattr(e, rec[0])(*a, **k)
            for (sem, val) in rec[3:]:
                ins.then_inc(sem, val)


def lazy(fn):
    fn._lazy = True
    return fn


class Sem:
    def __init__(self, h, name):
        self.h = h
        self.name = name
        self.n = 0


class Res:
    __slots__ = ("w", "r")

    def __init__(self):
        self.w = None
        self.r = {}


class Eng:
    def __init__(self, e, sem, is_pe=False):
        self.e = RecEngine()
        self.sem = sem
        self.is_pe = is_pe
        self.seen = {}

    def wait(self, sem, val):
        if val <= 0 or self.seen.get(sem.name, 0) >= val:
            return
        self.e.wait_ge(sem.h, val)
        self.seen[sem.name] = val

    def dep(self, tick, raw):
        sem, val = tick
        if sem is self.sem:
            if self.is_pe or not raw:
                return
            assert val <= sem.n, "same-engine dependency on a pending tick"
        self.wait(sem, val)


def op(E, fn, R=(), W=(), inc=True):
    for r in R:
        if r.w is not None:
            E.dep(r.w, True)
    for w in W:
        if w.w is not None:
            E.dep(w.w, False)
        for t in w.r.values():
            E.dep(t, False)
    ins = fn(E.e)
    if inc:
        ins.then_inc(E.sem.h, 1)
        E.sem.n += 1
        tick = (E.sem, E.sem.n)
    else:
        tick = (E.sem, E.sem.n + 1)
    for r in R:
        r.r[E.sem.name] = tick
    for w in W:
        w.w = tick
        w.r = {}
    return ins


def dma(Q, sem, out_ap, in_ap, R=(), W=()):
    for r in R:
        if r.w is not None:
            Q.dep(r.w, True)
    for w in W:
        if w.w is not None:
            Q.dep(w.w, True)
        for t in w.r.values():
            Q.dep(t, True)
    Q.e.dma_start(out=out_ap, in_=in_ap).then_inc(sem.h, 16)
    sem.n += 16
    tick = (sem, sem.n)
    for r in R:
        r.r[sem.name] = tick
    for w in W:
        w.w = tick
        w.r = {}


def build_program(mode):
    assert mode in ("fused", "l0", "l1")
    do_l0 = mode in ("fused", "l0")
    do_l1 = mode in ("fused", "l1")
    nc = bass.Bass("TRN2", target_bir_lowering=False)

    def din(name, shape, dt=F32):
        return nc.dram_tensor(name, list(shape), dt, kind="ExternalInput").ap()

    def dout(name, shape, dt=F32):
        return nc.dram_tensor(name, list(shape), dt, kind="ExternalOutput").ap()

    def dint(name, shape, dt):
        return nc.dram_tensor(name, list(shape), dt, kind="Internal").ap()

    c_ident = din("c_ident", [128, 128])
    c_ones16 = din("c_ones16", [128, 128], BF16)
    c_tri = din("c_tri", [128, 128])
    c_sel8 = din("c_sel8", [8, 128])
    c_gcol = din("c_gcol", [128, 6 * DC])
    c_bcol = din("c_bcol", [128, 6 * DC])
    c_gbc = din("c_gbc", [6, 2, 128, D])
    c_cw = din("c_cw", [128, DC * 3])
    c_fb = din("c_fb", [8, 1])
    c_core = din("c_core", [128, 8])
    c_bidx = din("c_bidx", [1, 4], I32)
    c_onescol = din("c_onescol", [128, 4])
    W1 = [[None, None], [None, None]]
    WD = [[None, None], [None, None]]
    for l in range(2):
        for k in range(2):
            if (l == 0 and do_l0) or (l == 1 and do_l1):
                W1[l][k] = din(f"w1_{l}{k}", [FC, 128, 2 * D])
                WD[l][k] = din(f"wd_{l}{k}", [FF, D])
    if do_l0:
        xin = din("xin", [NCOL, D])
        w_cin = din("w_cin", [DC, 128, 3 * D])
        w_cout = din("w_cout", [128, DC * D])
        w_kv = din("w_kv", [128, DC * 2056])
    if do_l1:
        w_q = din("w_q", [128, DC * D])
        w_o = din("w_o", [128, DC * D])
        out_d = dout("out", [NSLOT * SLOT, D])
    if mode == "l0":
        resid_x = dout("resid_x", [NTG * 128, D])
        hT_x = dout("hT_x", [128, DC * NCOL], BF16)
        kT_loc = dout("kT_loc", [D, NLOC], BF16)
        v_loc = dout("v_loc", [D, NTG * 128], BF16)
        lf_loc = dout("lf_loc", [8, NLOC])
    elif mode == "l1":
        resid_x = din("resid_x", [NTG * 128, D])
        hT_x = din("hT_x", [128, DC * NCOL], BF16)
        kT_loc = din("kT_loc", [D, NLOC], BF16)
        v_loc = din("v_loc", [D, NTG * 128], BF16)
        kTG = din("kTG", [8 * D, NLOC], BF16)
        vG = din("vG", [8 * D, NTG * 128], BF16)
        lfG = din("lfG", [64, NLOC])
    else:
        GR = 2 * D + 16
        gsrc = dint("gsrc", [GR, NTG * 128], BF16)
        gall = dint("gall", [8 * GR, NTG * 128], BF16)
        kT_loc = gsrc[0:D, 0:NLOC]
        v_loc = gsrc[D:2 * D, :]
        lf_loc = gsrc[2 * D:GR, :].rearrange("r c -> (r c)").bitcast(F32)[0:8 * NLOC].rearrange("(h t) -> h t", h=8)

    if mode == "fused":
        w1b = dint("w1b", [FC, 128, 2 * D], BF16)
        wdb = dint("wdb", [FF, D], BF16)
    if do_l1:
        kTS = dint("kTS", [4 * D, NLOC], BF16)
        vS = dint("vS", [4 * D, NTG * 128], BF16)
        lfS = dint("lfS", [32, NLOC], F32) if mode == "l1" else dint("lfS", [64, NTG * 128], BF16)

    with ExitStack() as es:
        def sb(name, shape, dt):
            return es.enter_context(nc.sbuf_tensor(name, list(shape), dt))

        nsem = [0]

        def mksem(name):
            nsem[0] += 1
            return Sem(es.enter_context(nc.semaphore(name)), name)

        resid = sb("resid", [128, NTG, D], F32)
        hT = sb("hT", [128, DC, NCOL], BF16)
        BIGB = 76480
        big = sb("big", [128, BIGB], U8)
        ybuf = sb("ybuf", [128, D], F32)
        nbufs = [sb("nbuf0", [128, D], F32), sb("nbuf1", [128, D], F32)]
        gbc = sb("gbc", [128, 2, D], F32)
        sg = sb("sg", [128, SLOT], F32)
        halo_tmp = sb("halo_tmp", [HALO, D], F32)
        ident = sb("ident", [128, 128], F32)
        ones16 = sb("ones16", [128, 128], BF16)
        tri = sb("tri", [128, 128], F32)
        sel8 = sb("sel8", [8, 128], F32)
        gcol = sb("gcol", [128, 6 * DC], F32)
        bcol = sb("bcol", [128, 6 * DC], F32)
        cw = sb("cw", [128, DC * 3], F32)
        fb = sb("fb", [8, 1], F32)
        ccore = sb("ccore", [128, 8], F32)
        bidx = sb("bidx", [1, 4], I32)
        onescol = sb("onescol", [128, 4], F32)
        st = sb("st", [128, 12], F32)
        mv = sb("mv", [128, 2], F32)
        sd = sb("sd", [128, 1], F32)
        rs = sb("rs", [128, 1], F32)
        ps = es.enter_context(nc.psum_tensor("ps", [128, 8, 512], F32))

        def carve(off, nbytes, dt, pattern=None, **kw):
            ap = big[:, off:off + nbytes].bitcast(dt)
            if pattern:
                ap = ap.rearrange(pattern, **kw)
            return ap

        s_pe = mksem("s_pe")
        s_dve = mksem("s_dve")
        s_act = mksem("s_act")
        s_pool = mksem("s_pool")
        s_cc = mksem("s_cc")
        dq = {n: mksem("d_" + n) for n in ("cv", "kvs", "misc", "x", "x0", "x1", "x2", "x3", "xh", "w1a", "w1b", "wd", "wm",
                                            "kk0", "kk1", "kk2", "kk3", "kv0", "kv1", "kv2", "kv3",
                                            "stk0", "stk1", "stv0", "stv1", "stf", "out", "gb")}
        block = es.enter_context(nc.Block())
        engs = {}

        PE = Eng(nc.tensor, s_pe, is_pe=True)
        DVE = Eng(nc.vector, s_dve)
        ACT = Eng(nc.scalar, s_act)
        QW = Eng(nc.gpsimd, s_pool)
        QS = Eng(nc.sync, None)

        bank = [ps[:, b, :] for b in range(8)]
        bres = [Res() for _ in range(8)]

        r_resid = [Res() for _ in range(NTG)]
        r_hm = [Res() for _ in range(NSLOT)]
        r_hh = [Res() for _ in range(NSLOT)]
        r_ybuf, r_gbc, r_sg, r_halo = Res(), Res(), Res(), Res()
        r_nbufs = [Res(), Res()]
        ln_cnt = [0]
        pending = []

        def flush():
            while pending:
                pending.pop(0)()
        r_st, r_mv, r_sd, r_rs = Res(), Res(), Res(), Res()
        r_const = Res()

        for dst, src in ((ident, c_ident), (ones16, c_ones16), (tri, c_tri), (sel8, c_sel8), (gcol, c_gcol),
                         (bcol, c_bcol), (cw, c_cw), (fb, c_fb), (ccore, c_core), (bidx, c_bidx), (onescol, c_onescol)):
            dma(QS, dq["misc"], dst[:], src, W=[r_const])

        def tg_info(m):
            if m == 16:
                return HALO, 0, 0, True
            i, q = divmod(m, 4)
            return 128, SW * i + HALO + 128 * q, i, False

        tp_rr = [0]

        def transpose_evac(src_ap, r_src, ntok, c0, r_dst, ln_idx):
            pa = (tp_rr[0] % 2) * 2
            tp_rr[0] += 1
            for dc in range(DC):
                b = pa + dc // 4
                o = bank[b][:, (dc % 4) * 128:(dc % 4) * 128 + ntok]
                op(PE, lambda e, o=o, dc=dc: e.transpose(o, src_ap[0:ntok, dc * 128:(dc + 1) * 128], ident[0:ntok, 0:ntok]),
                   R=[r_src, r_const], W=[bres[b]], inc=(dc % 4 == 3))
            for dc in range(DC):
                b = pa + dc // 4
                i_ap = bank[b][:, (dc % 4) * 128:(dc % 4) * 128 + ntok]
                o_ap = hT[:, dc, c0:c0 + ntok]
                if ln_idx is None:
                    op(ACT, lambda e, o_ap=o_ap, i_ap=i_ap: e.activation(o_ap, i_ap, AF.Copy), R=[bres[b]], W=[r_dst])
                else:
                    k = ln_idx * DC + dc
                    op(ACT, lambda e, o_ap=o_ap, i_ap=i_ap, k=k: e.activation(
                        o_ap, i_ap, AF.Identity, bias=bcol[:, k:k + 1], scale=gcol[:, k:k + 1]),
                       R=[bres[b], r_const], W=[r_dst])

        cur_ln = [None]

        def load_gb(ln_idx):
            if cur_ln[0] == ln_idx:
                return
            cur_ln[0] = ln_idx
            dma(QS, dq["gb"], gbc[:, 0, :], c_gbc[ln_idx, 0], W=[r_gbc])
            dma(QS, dq["gb"], gbc[:, 1, :], c_gbc[ln_idx, 1], W=[r_gbc])

        def layer_norm(b0, b1, kscale, ln_idx, res_ap, r_res, ntok, c0, r_dst, want_hT=True):
            load_gb(ln_idx)
            nbuf = nbufs[ln_cnt[0] % 2]
            r_nbuf = r_nbufs[ln_cnt[0] % 2]
            ln_cnt[0] += 1
            for half, b in ((0, b0), (1, b1)):
                hs = slice(half * 512, half * 512 + 512)
                op(DVE, lambda e, b=b, hs=hs: e.scalar_tensor_tensor(
                    ybuf[0:ntok, hs], bank[b][0:ntok, :], float(kscale), res_ap[0:ntok, hs], ALU.mult, ALU.add),
                   R=[bres[b], r_res], W=[r_ybuf])
            for half in range(2):
                hs = slice(half * 512, half * 512 + 512)
                op(DVE, lambda e, half=half, hs=hs: e.bn_stats(st[0:ntok, half * 6:half * 6 + 6], ybuf[0:ntok, hs]),
                   R=[r_ybuf], W=[r_st])
            op(DVE, lambda e: e.bn_aggr(mv[0:ntok, :], st[0:ntok, :]), R=[r_st], W=[r_mv])
            op(ACT, lambda e: e.activation(sd[0:ntok, :], mv[0:ntok, 1:2], AF.Sqrt, bias=float(EPS_P), scale=1.0),
               R=[r_mv], W=[r_sd])
            op(DVE, lambda e: e.reciprocal(rs[0:ntok, :], sd[0:ntok, :]), R=[r_sd], W=[r_rs])
            op(DVE, lambda e: e.tensor_scalar(nbuf[0:ntok, :], ybuf[0:ntok, :], mv[0:ntok, 0:1], rs[0:ntok, 0:1],
                                              ALU.subtract, ALU.mult),
               R=[r_ybuf, r_mv, r_rs], W=[r_nbuf])
            op(DVE, lambda e: e.tensor_tensor(res_ap[0:ntok, :], nbuf[0:ntok, :], gbc[0:ntok, 0, :], ALU.mult),
               R=[r_nbuf, r_gbc], W=[r_res])
            op(DVE, lambda e: e.tensor_tensor(res_ap[0:ntok, :], res_ap[0:ntok, :], gbc[0:ntok, 1, :], ALU.add),
               R=[r_res, r_gbc], W=[r_res])
            if want_hT:
                pending.append(lambda: transpose_evac(nbuf, r_nbuf, ntok, c0, r_dst, ln_idx))
                if not DELAY_T:
                    flush()

        w1ring = [carve(k * 4096, 4096, BF16, "p (t c f) -> p t c f", t=2, c=DC) for k in range(NW1)]
        r_w1 = [Res() for _ in range(NW1)]
        WD_OFF = NW1 * 4096
        wd_sb = carve(WD_OFF, FC * D * 2, BF16, "p (f d) -> p f d", f=FC)
        r_wd = [Res() for _ in range(11)]
        AT_OFF = WD_OFF + FC * D * 2
        aT = carve(AT_OFF, FC * SW * 2, BF16, "p (f c) -> p f c", f=FC)
        r_aT = Res()
        assert AT_OFF + FC * SW * 2 <= BIGB
        w1_cnt = [0]
        p1_rr = [0]
        p2_rr = [0]

        def ffn(l, k, ln_idx, halo_mode, final=False, src=None):
            w1d, wdd = W1[l][k], WD[l][k]
            Qw, xR = QW, []
            if src is not None:
                w1d, wdd, Qw, xR = src
            wdv = wdd.rearrange("(f p) d -> p f d", p=128)
            total = NSLOT * FC
            issued = [0]

            def issue_w1():
                g = issued[0]
                if g >= total:
                    return
                issued[0] += 1
                slot = w1_cnt[0] % NW1
                w1_cnt[0] += 1
                dma(Qw, dq["w1a" if slot == 0 else "w1b"], w1ring[slot].rearrange("p t c f -> p (t c f)"),
                    w1d[g % FC], R=xR, W=[r_w1[slot]])
                return slot

            slots_of = {}
            for _ in range(NW1):
                g = issued[0]
                slots_of[g] = issue_w1()
            for c in range(11):
                dma(Qw, dq["wd"], wd_sb[:, 2 * c:2 * c + 2, :], wdv[:, 2 * c:2 * c + 2, :], R=xR, W=[r_wd[c]])
            for c in range(11):
                r_wd[c].w = (dq["wd"], dq["wd"].n)
            for i in range(NSLOT):
                subs = []
                has_halo = halo_mode == "all" or (halo_mode == "h0" and i == 0)
                if has_halo:
                    subs.append((SW * i, HALO, r_hh[i], 0))
                subs.append((SW * i + HALO, SLOT, r_hm[i], HALO))
                if halo_mode == "all" and i > 0:
                    dma(QS, dq["xh"], halo_tmp[:], xin[SW * i:SW * i + HALO, :], W=[r_halo])
                for f in range(FC):
                    g = i * FC + f
                    slot = slots_of.pop(g)
                    wr = w1ring[slot]
                    for (c0, n, r_h, lc) in subs:
                        pb = (p1_rr[0] % 2) * 2
                        p1_rr[0] += 1
                        for t in range(2):
                            for dc in range(DC):
                                op(PE, lambda e, t=t, dc=dc, pb=pb, c0=c0, n=n: e.matmul(
                                    bank[pb + t][:, 0:n], wr[:, t, dc, :], hT[:, dc, c0:c0 + n],
                                    start=(dc == 0), stop=(dc == DC - 1)),
                                   R=[r_w1[slot], r_h], W=[bres[pb + t]], inc=(dc == DC - 1))
                        op(ACT, lambda e, pb=pb, n=n: e.activation(sg[:, 0:n], bank[pb][:, 0:n], AF.Silu),
                           R=[bres[pb]], W=[r_sg])
                        op(DVE, lambda e, pb=pb, n=n, lc=lc, f=f: e.tensor_tensor(
                            aT[:, f, lc:lc + n], sg[:, 0:n], bank[pb + 1][:, 0:n], ALU.mult),
                           R=[r_sg, bres[pb + 1]], W=[r_aT])
                    flush()
                    g2 = issued[0]
                    s2 = issue_w1()
                    if s2 is not None:
                        slots_of[g2] = s2
                tgs = []
                if has_halo:
                    if i == 0:
                        tgs.append((HALO, 0, SW * i, resid[:, 16, :], r_resid[16], r_hh[i]))
                    else:
                        tgs.append((HALO, 0, SW * i, halo_tmp, r_halo, r_hh[i]))
                for q in range(4):
                    m = 4 * i + q
                    tgs.append((128, HALO + 128 * q, SW * i + HALO + 128 * q, resid[:, m, :], r_resid[m], r_hm[i]))
                for (ntok, lc, c0, res_ap, r_res, r_dst) in tgs:
                    pb = 4 + (p2_rr[0] % 2) * 2
                    p2_rr[0] += 1
                    for half in range(2):
                        for f in range(FC):
                            op(PE, lambda e, half=half, f=f, pb=pb, lc=lc, ntok=ntok: e.matmul(
                                bank[pb + half][0:ntok, :], aT[:, f, lc:lc + ntok], wd_sb[:, f, half * 512:half * 512 + 512],
                                start=(f == 0), stop=(f == FC - 1)),
                               R=[r_aT, r_wd[f // 2]], W=[bres[pb + half]], inc=(f == FC - 1))
                    flush()
                    layer_norm(pb, pb + 1, K_FFN, ln_idx, res_ap, r_res, ntok, c0, r_dst, want_hT=not final)
            flush()

        if do_l0:
            dma(QS, dq["x0"], resid[0:HALO, 16, :], xin[0:HALO, :], W=[r_resid[16]])
            for m in range(16):
                i, q = divmod(m, 4)
                r0 = SW * i + HALO + 128 * q
                dma(QS, dq["x%d" % i], resid[:, m, :], xin[r0:r0 + 128, :], W=[r_resid[m]])
            for m in range(17):
                i = 0 if m == 16 else m // 4
                r_resid[m].w = (dq["x%d" % i], dq["x%d" % i].n)
            for i in range(NSLOT):
                if i == 0:
                    transpose_evac(resid[:, 16, :], r_resid[16], HALO, 0, r_hh[0], None)
                else:
                    dma(QS, dq["xh"], halo_tmp[:], xin[SW * i:SW * i + HALO, :], W=[r_halo])
                    transpose_evac(halo_tmp, r_halo, HALO, SW * i, r_hh[i], None)
                for q in range(4):
                    m = 4 * i + q
                    transpose_evac(resid[:, m, :], r_resid[m], 128, SW * i + HALO + 128 * q, r_hm[i], None)

            ffn(0, 0, 0, "all")

            wc_ring = [carve(k * 6144, 6144, BF16, "p (c w f) -> p c w f", c=DC, w=3) for k in range(2)]
            r_wc = [Res(), Res()]
            WO_OFF = 2 * 6144
            wout_sb = carve(WO_OFF, DC * D * 2, BF16, "p (c d) -> p c d", c=DC)
            r_wout = Res()
            off = WO_OFF + DC * D * 2
            u_sb = carve(off, 532 * 4, F32)
            off += 532 * 4
            cg_sb = carve(off, SW * 4, F32)
            off += SW * 4
            y_sb = carve(off, SW * 4, F32)
            off += SW * 4
            zT = carve(off, DC * SW * 2, BF16, "p (c n) -> p c n", c=DC)
            off += DC * SW * 2
            assert off <= BIGB
            r_u, r_cg, r_y, r_zT = Res(), Res(), Res(), Res()
            alias = [r_aT] + r_wd + r_w1

            def wait_alias(E, lst):
                for r in lst:
                    if r.w is not None:
                        E.dep(r.w, True)
                    for t in r.r.values():
                        E.dep(t, True)

            wait_alias(QW, alias)
            wait_alias(DVE, alias)
            wait_alias(ACT, alias)
            for c in range(4):
                dma(QW, dq["wm"], wout_sb[:, 2 * c:2 * c + 2, :],
                    w_cout[:, 2 * c * D:(2 * c + 2) * D].rearrange("p (c d) -> p c d", c=2), W=[r_wout])
            op(DVE, lambda e: e.memset(u_sb[:, 0:2], 0.0), W=[r_u])
            wc_cnt = 0
            r_conv = Res()
            conv_jobs = []
            if mode == "fused":
                for f in range(FC):
                    conv_jobs.append((w1b[f], W1[1][0][f]))
                for c in range(11):
                    conv_jobs.append((wdb[256 * c:256 * (c + 1), :], WD[1][0][256 * c:256 * (c + 1), :]))
            for i in range(NSLOT):
                c0h, c0m = SW * i, SW * i + HALO
                for dcn in range(DC):
                    slot = wc_cnt % 2
                    wc_cnt += 1
                    dma(QW, dq["w1a" if slot == 0 else "w1b"], wc_ring[slot].rearrange("p c w f -> p (c w f)"),
                        w_cin[dcn], W=[r_wc[slot]])
                    for _ in range(2 if (i == NSLOT - 1 and dcn == DC - 1) else 1):
                        if conv_jobs:
                            cdst, csrc = conv_jobs.pop(0)
                            dma(QW, dq["cv"], cdst, csrc, W=[r_conv])
                    wr = wc_ring[slot]
                    base = 0 if dcn % 2 == 0 else 4
                    for w in range(3):
                        for dc in range(DC):
                            op(PE, lambda e, w=w, dc=dc: e.matmul(
                                bank[base + 3][:, w * HALO:(w + 1) * HALO], wr[:, dc, w, :], hT[:, dc, c0h:c0h + HALO],
                                start=(dc == 0), stop=(dc == DC - 1)),
                               R=[r_wc[slot], r_hh[i]], W=[bres[base + 3]], inc=(dc == DC - 1))
                        for dc in range(DC):
                            op(PE, lambda e, w=w, dc=dc: e.matmul(
                                bank[base + w][:, :], wr[:, dc, w, :], hT[:, dc, c0m:c0m + SLOT],
                                start=(dc == 0), stop=(dc == DC - 1)),
                               R=[r_wc[slot], r_hm[i]], W=[bres[base + w]], inc=(dc == DC - 1))
                    op(DVE, lambda e: e.tensor_copy(cg_sb[:, 0:HALO], bank[base + 3][:, HALO:2 * HALO]),
                       R=[bres[base + 3]], W=[r_cg])
                    op(ACT, lambda e: e.activation(cg_sb[:, HALO:SW], bank[base + 1][:, :], AF.Copy),
                       R=[bres[base + 1]], W=[r_cg])
                    op(DVE, lambda e: e.tensor_tensor(u_sb[:, 2:2 + HALO], cg_sb[:, 0:HALO],
                                                      bank[base + 3][:, 2 * HALO:3 * HALO], ALU.mult),
                       R=[r_cg, bres[base + 3]], W=[r_u])
                    op(DVE, lambda e: e.tensor_tensor(u_sb[:, 2 + HALO:2 + SW], cg_sb[:, HALO:SW], bank[base + 2][:, :],
                                                      ALU.mult),
                       R=[r_cg, bres[base + 2]], W=[r_u])
                    k3 = dcn * 3
                    op(DVE, lambda e: e.tensor_scalar(y_sb[:, 0:SW], u_sb[:, 2:2 + SW], cw[:, k3 + 2:k3 + 3], None, ALU.mult),
                       R=[r_u, r_const], W=[r_y])
                    op(DVE, lambda e: e.scalar_tensor_tensor(y_sb[:, 0:SW], u_sb[:, 1:1 + SW], cw[:, k3 + 1:k3 + 2],
                                                             y_sb[:, 0:SW], ALU.mult, ALU.add),
                       R=[r_u, r_y], W=[r_y])
                    op(DVE, lambda e: e.scalar_tensor_tensor(y_sb[:, 0:SW], u_sb[:, 0:SW], cw[:, k3:k3 + 1],
                                                             y_sb[:, 0:SW], ALU.mult, ALU.add),
                       R=[r_u, r_y], W=[r_y])
                    op(DVE, lambda e: e.tensor_tensor(zT[:, dcn, 0:HALO], y_sb[:, 0:HALO], bank[base + 3][:, 0:HALO],
                                                      ALU.mult),
                       R=[r_y, bres[base + 3]], W=[r_zT])
                    op(DVE, lambda e: e.tensor_tensor(zT[:, dcn, HALO:SW], y_sb[:, HALO:SW], bank[base][:, :], ALU.mult),
                       R=[r_y, bres[base]], W=[r_zT])
                    flush()
                tgs = []
                if i == 0:
                    tgs.append((HALO, 0, 0, resid[:, 16, :], r_resid[16], r_hh[0]))
                for q in range(4):
                    m = 4 * i + q
                    tgs.append((128, HALO + 128 * q, SW * i + HALO + 128 * q, resid[:, m, :], r_resid[m], r_hm[i]))
                for (ntok, lc, c0, res_ap, r_res, r_dst) in tgs:
                    pb = 4 + (p2_rr[0] % 2) * 2
                    p2_rr[0] += 1
                    for half in range(2):
                        for dc in range(DC):
                            op(PE, lambda e, half=half, dc=dc: e.matmul(
                                bank[pb + half][0:ntok, :], zT[:, dc, lc:lc + ntok],
                                wout_sb[:, dc, half * 512:half * 512 + 512], start=(dc == 0), stop=(dc == DC - 1)),
                               R=[r_zT, r_wout], W=[bres[pb + half]], inc=(dc == DC - 1))
                    flush()
                    layer_norm(pb, pb + 1, K_MIX, 1, res_ap, r_res, ntok, c0, r_dst)
            flush()
            alias_c = [r_wc[0], r_wc[1], r_wout, r_u, r_cg, r_y, r_zT]
            wait_alias(QW, alias_c)
            wait_alias(DVE, alias_c)

            ffn(0, 1, 2, "h0")

            alias = [r_aT] + r_wd + r_w1
            wait_alias(QW, alias)
            wait_alias(DVE, alias)
            wait_alias(ACT, alias)
            wkv = carve(0, DC * 2056 * 2, BF16, "p (c n) -> p c n", c=DC)
            r_wkv = Res()
            off = DC * 2056 * 2
            kT_sb = [carve(off + k * SW * 2, SW * 2, BF16) for k in range(2)]
            off += 2 * SW * 2
            v_sb = [carve(off + k * D * 2, D * 2, BF16) for k in range(2)]
            off += 2 * D * 2
            fl = [carve(off + k * SW * 4, SW * 4, F32) for k in range(5)]
            off += 5 * SW * 4
            assert off <= BIGB
            r_kT = [Res(), Res()]
            r_v = [Res(), Res()]
            r_fl = [Res() for _ in range(5)]
            for dc in range(DC):
                dma(QW, dq["wm"], wkv[:, dc, :], w_kv[:, dc * 2056:(dc + 1) * 2056], W=[r_wkv])
            kcnt = 0
            vcnt = 0
            krr = 0
            v3 = v_loc.rearrange("(h p) (t e) -> p h t e", h=NH, t=NTG)
            for i in range(NSLOT):
                subs = []
                if i == 0:
                    subs.append((0, HALO, r_hh[0], 0, 0))
                subs.append((SW * i + HALO, SLOT, r_hm[i], HALO if i == 0 else 0, HALO + SLOT * i))
                for hc in range(NH):
                    kb = kT_sb[kcnt % 2]
                    r_kb = r_kT[kcnt % 2]
                    kcnt += 1
                    for (c0, n, r_h, lc, gcol0) in subs:
                        b = krr % 4
                        krr += 1
                        for dc in range(DC):
                            op(PE, lambda e, dc=dc: e.matmul(bank[b][:, 0:n], wkv[:, dc, hc * 128:(hc + 1) * 128],
                                                             hT[:, dc, c0:c0 + n], start=(dc == 0), stop=(dc == DC - 1)),
                               R=[r_wkv, r_h], W=[bres[b]], inc=(dc == DC - 1))
                        op(ACT, lambda e: e.activation(kb[:, lc:lc + n], bank[b][:, 0:n], AF.Copy), R=[bres[b]], W=[r_kb])
                    ntot = sum(s[1] for s in subs)
                    g0 = subs[0][4]
                    dma(QS, dq["stk%d" % ((kcnt - 1) % 2)], kT_loc[hc * 128:(hc + 1) * 128, g0:g0 + ntot], kb[:, 0:ntot], R=[r_kb])
                for (c0, n, r_h, lc, gcol0) in subs:
                    b = krr % 4
                    krr += 1
                    for dc in range(DC):
                        op(PE, lambda e, dc=dc: e.matmul(bank[b][0:8, 0:n], wkv[:, dc, 2048:2056], hT[:, dc, c0:c0 + n],
                                                         start=(dc == 0), stop=(dc == DC - 1)),
                           R=[r_wkv, r_h], W=[bres[b]], inc=(dc == DC - 1))
                    x_, a_, e_, l_, o_ = [t[0:8, 0:n] for t in fl]
                    op(ACT, lambda e: e.activation(x_, bank[b][0:8, 0:n], AF.Identity, bias=fb[0:8, 0:1], scale=1.0),
                       R=[bres[b], r_const], W=[r_fl[0]])
                    op(DVE, lambda e: e.scalar_tensor_tensor(a_, x_, -1.0, x_, ALU.mult, ALU.max), R=[r_fl[0]], W=[r_fl[1]])
                    op(ACT, lambda e: e.activation(e_, a_, AF.Exp, scale=-1.0), R=[r_fl[1]], W=[r_fl[2]])
                    op(ACT, lambda e: e.activation(l_, e_, AF.Ln, bias=1.0, scale=1.0), R=[r_fl[2]], W=[r_fl[3]])
                    op(DVE, lambda e: e.scalar_tensor_tensor(o_, x_, 0.0, l_, ALU.min, ALU.subtract),
                       R=[r_fl[0], r_fl[3]], W=[r_fl[4]])
                    dma(QS, dq["stf"], lf_loc[0:8, gcol0:gcol0 + n], o_, R=[r_fl[4]])
                tgl = ([16] if i == 0 else []) + [4 * i + q for q in range(4)]
                for m in tgl:
                    ntok, c0, _, _ = tg_info(m)
                    r_h = r_hh[0] if m == 16 else r_hm[i]
                    vb = v_sb[vcnt % 2]
                    r_vb = r_v[vcnt % 2]
                    vcnt += 1
                    pb = 4 + (p2_rr[0] % 2) * 2
                    p2_rr[0] += 1
                    for half in range(2):
                        for dc in range(DC):
                            op(PE, lambda e, half=half, dc=dc: e.matmul(
                                bank[pb + half][0:ntok, :], hT[:, dc, c0:c0 + ntok],
                                wkv[:, dc, 1024 + half * 512:1024 + half * 512 + 512], start=(dc == 0), stop=(dc == DC - 1)),
                               R=[r_wkv, r_h], W=[bres[pb + half]], inc=(dc == DC - 1))
                        op(DVE, lambda e, half=half: e.tensor_copy(vb[0:ntok, half * 512:half * 512 + 512],
                                                                   bank[pb + half][0:ntok, :]),
                           R=[bres[pb + half]], W=[r_vb])
                    dma(QS, dq["stv%d" % ((vcnt - 1) % 2)], v3[0:ntok, :, m, :], vb[0:ntok, :].rearrange("p (h e) -> p h e", h=NH), R=[r_vb])
            alias_e = [r_wkv] + r_kT + r_v + r_fl
            if mode == "l0":
                for m in range(NTG):
                    ntok = HALO if m == 16 else 128
                    dma(QS, dq["out"], resid_x[m * 128:m * 128 + ntok, :], resid[0:ntok, m, :], R=[r_resid[m]])
                for dc in range(DC):
                    dma(QS, dq["out"], hT_x[:, dc * NCOL:(dc + 1) * NCOL], hT[:, dc, :], R=r_hm + r_hh)
                QS.wait(dq["out"], dq["out"].n)
                for nm in ("stk0", "stk1", "stv0", "stv1", "stf"):
                    QS.wait(dq[nm], dq[nm].n)
            else:
                for nm in ("stk0", "stk1", "stv0", "stv1", "stf"):
                    QW.wait(dq[nm], dq[nm].n)
                QW.e.collective_compute("AllGather", ALU.bypass, replica_groups=[list(range(8))],
                                        ins=[gsrc.opt()], outs=[gall.opt()]).then_inc(s_cc.h, 1)
                s_cc.n += 1
        else:
            alias_e = []

        if do_l1:
            def wait_alias(E, lst):
                for r in lst:
                    if r.w is not None:
                        E.dep(r.w, True)
                    for t in r.r.values():
                        E.dep(t, True)

            if mode == "l1":
                for m in range(16):
                    dma(QS, dq["x"], resid[:, m, :], resid_x[m * 128:(m + 1) * 128, :], W=[r_resid[m]])
                for dc in range(DC):
                    dma(QS, dq["x"], hT[:, dc, :], hT_x[:, dc * NCOL:(dc + 1) * NCOL], W=r_hm + r_hh)
            else:
                wait_alias(QW, alias_e)
                wait_alias(QS, alias_e)
                wait_alias(DVE, alias_e)
                wait_alias(ACT, alias_e)

            load_gb(3)
            dynctx = {}
            QC = QW if mode == "fused" else QS
            QC.wait(dq["misc"], dq["misc"].n)
            QC.e.special("breg")
            r_kvs = Res()
            if mode == "fused":
                QW.wait(s_cc, 1)
                G4 = gall.rearrange("(b r) c -> b r c", b=2)

                def dyn(rows, cols):
                    return lazy(lambda: G4[bass.ds(dynctx["bval"], 1), rows, cols].rearrange("a r c -> (a r) c"))

                for jp in range(4):
                    for hf in range(2):
                        r0 = jp * GR + 512 * hf
                        d0 = jp * D + 512 * hf
                        dma(QW, dq["kvs"], kTS[d0:d0 + 512, :], dyn(slice(r0, r0 + 512), slice(0, NLOC)), W=[r_kvs])
                        dma(QW, dq["kvs"], vS[d0:d0 + 512, :], dyn(slice(r0 + D, r0 + D + 512), slice(0, NTG * 128)),
                            W=[r_kvs])
                dma(QW, dq["kvs"], lfS.rearrange("(j r) c -> j r c", j=4),
                    lazy(lambda: G4[bass.ds(dynctx["bval"], 1), :, :].rearrange("a r c -> (a r) c")
                         .rearrange("(j r) c -> j r c", j=4)[:, 2 * D:GR, :]), W=[r_kvs])
                lfSv = [lfS[16 * jp:16 * (jp + 1), :].rearrange("r c -> (r c)").bitcast(F32)[0:8 * NLOC]
                        .rearrange("(h t) -> h t", h=8) for jp in range(4)]
            else:
                kTG4 = kTG.rearrange("(b r) c -> b r c", b=2)
                vG4 = vG.rearrange("(b r) c -> b r c", b=2)
                lfG4 = lfG.rearrange("(b r) c -> b r c", b=2)

                def dyn(ap3, rows):
                    return lazy(lambda: ap3[bass.ds(dynctx["bval"], 1), rows, :].rearrange("a r c -> (a r) c"))

                for c in range(8):
                    rows = slice(512 * c, 512 * (c + 1))
                    dma(QS, dq["kvs"], kTS[rows, :], dyn(kTG4, rows), W=[r_kvs])
                    dma(QS, dq["kvs"], vS[rows, :], dyn(vG4, rows), W=[r_kvs])
                dma(QS, dq["kvs"], lfS[:, :], dyn(lfG4, slice(0, 32)), W=[r_kvs])
                lfSv = [lfS[8 * jp:8 * jp + 8, :] for jp in range(4)]

            if mode == "fused":
                assert not conv_jobs
                ffn(1, 0, 3, "none", src=(w1b, wdb, QS, [r_conv]))
            else:
                ffn(1, 0, 3, "none")

            alias = [r_aT] + r_wd + r_w1
            for E in (QW, QS, DVE, ACT):
                wait_alias(E, alias)
            off = 0
            wqo = carve(off, DC * D * 2, BF16, "p (c d) -> p c d", c=DC)
            off += DC * D * 2
            qT_all = carve(off, NH * SLOT * 2, BF16, "p (h n) -> p h n", h=NH)
            off += NH * SLOT * 2
            cqb = [carve(off + k * SLOT * 4, SLOT * 4, F32) for k in range(2)]
            off += 2 * SLOT * 4
            cq_own = carve(off, NSLOT * SLOT * 4, F32, "p (i n) -> p i n", i=NSLOT)
            off += NSLOT * SLOT * 4
            NB = 65
            negck = carve(off, NB * NH * 4, F32, "p (g h) -> p g h", g=NB)
            off += NB * NH * 4
            negck_grp = carve(off, NSLOT * 4 * 4 * NH * 4, F32, "p (i j b h) -> p i j b h", i=NSLOT, j=4, b=4)
            off += NSLOT * 4 * 4 * NH * 4
            negck_own = carve(off, NSLOT * 4 * NH * 4, F32, "p (i b h) -> p i b h", i=NSLOT, b=4)
            off += NSLOT * 4 * NH * 4
            carry = carve(off, 32, F32)
            off += 32
            cqm = carve(off, SLOT * 4, F32)
            off += SLOT * 4
            r_cqm = Res()
            A0 = off
            CH = HALO + 4 * SLOT
            lfc = carve(A0, CH * 4, F32)
            chm = carve(A0 + CH * 4, CH * 4, F32)
            aoT = carve(off, NH * SLOT * 2, BF16, "p (h n) -> p h n", h=NH)
            off += NH * SLOT * 2
            kring = [carve(off + k * SLOT * 2, SLOT * 2, BF16) for k in range(NKV)]
            off += NKV * SLOT * 2
            vring = [carve(off + k * SLOT * 2, SLOT * 2, BF16, "p (b e) -> p b e", b=4) for k in range(NKV)]
            off += NKV * SLOT * 2
            tbuf = [carve(off + k * SLOT * 4, SLOT * 4, F32) for k in range(3)]
            off += 3 * SLOT * 4
            pbuf = [carve(off + k * SLOT * 2, SLOT * 2, BF16) for k in range(5)]
            off += 5 * SLOT * 2
            rsum = carve(off, SLOT * 4, F32)
            off += SLOT * 4
            off = max(off, A0 + 2 * CH * 4)
            assert off <= BIGB, off
            r_wqo, r_aoT, r_qT = Res(), Res(), Res()
            r_kr = [Res() for _ in range(NKV)]
            r_vr = [Res() for _ in range(NKV)]
            r_tb = [Res() for _ in range(3)]
            r_pb = [Res() for _ in range(5)]
            r_cqb = [Res(), Res()]
            r_rsum, r_lfc, r_chm, r_cqo, r_nck, r_nckg, r_ncko, r_carry = (Res() for _ in range(8))

            for ci in range(NSLOT):
                if ci == 0:
                    dma(QS, dq["misc"], lfc[0:8, 0:HALO], lfSv[0][0:8, 0:HALO], R=[r_kvs], W=[r_lfc])
                for jp in range(4):
                    dma(QS, dq["misc"], lfc[0:8, HALO + SLOT * jp:HALO + SLOT * (jp + 1)],
                        lfSv[jp][0:8, HALO + SLOT * ci:HALO + SLOT * (ci + 1)], R=[r_kvs], W=[r_lfc])
                lo = 0 if ci == 0 else HALO
                n = CH - lo
                init = 0.0 if ci == 0 else carry[0:8, 0:1]
                op(DVE, lambda e: e.tensor_tensor_scan(chm[0:8, lo:CH], onescol[0:8, 0:1].to_broadcast([8, n]),
                                                       lfc[0:8, lo:CH], init, ALU.mult, ALU.add),
                   R=[r_lfc, r_carry, r_const], W=[r_chm])
                op(DVE, lambda e: e.tensor_copy(carry[0:8, 0:1], chm[0:8, CH - 1:CH]), R=[r_chm], W=[r_carry])
                op(DVE, lambda e: e.tensor_scalar(cq_own[0:8, ci, :], chm[0:8, HALO:HALO + SLOT], ccore[0:8, 0:1], None,
                                                  ALU.mult), R=[r_chm, r_const], W=[r_cqo])
                for jp in range(1, 4):
                    op(DVE, lambda e: e.scalar_tensor_tensor(
                        cq_own[0:8, ci, :], chm[0:8, HALO + SLOT * jp:HALO + SLOT * (jp + 1)], ccore[0:8, jp:jp + 1],
                        cq_own[0:8, ci, :], ALU.mult, ALU.add), R=[r_chm, r_cqo, r_const], W=[r_cqo])
                tb_ = 3
                nblk = 16
                for k in range(nblk):
                    op(PE, lambda e: e.transpose(bank[tb_][:, k * 8:k * 8 + 8], chm[0:8, HALO + 128 * k:HALO + 128 * (k + 1)],
                                                 ident[0:8, 0:8]), R=[r_chm, r_const], W=[bres[tb_]],
                       inc=(k == nblk - 1 and ci != 0))
                if ci == 0:
                    op(PE, lambda e: e.transpose(bank[tb_][0:HALO, 128:136], chm[0:8, 0:HALO], ident[0:8, 0:8]),
                       R=[r_chm, r_const], W=[bres[tb_]])
                    op(DVE, lambda e: e.tensor_scalar(negck[0:HALO, 0, :], bank[tb_][0:HALO, 128:136], -1.0, None, ALU.mult),
                       R=[bres[tb_]], W=[r_nck])
                op(DVE, lambda e: e.tensor_scalar(
                    negck[:, 1 + 16 * ci:17 + 16 * ci, :].rearrange("p g h -> p (g h)"), bank[tb_][:, 0:128], -1.0, None,
                    ALU.mult), R=[bres[tb_]], W=[r_nck])
            nck_x = negck[:, 1:65, :].rearrange("p (i j b) h -> p i j b h", i=4, j=4, b=4)
            for jp in range(4):
                op(DVE, lambda e: e.tensor_scalar(negck_grp[:, :, jp, :, :], nck_x[:, :, jp, :, :], ccore[:, 4 + jp:5 + jp],
                                                  None, ALU.add), R=[r_nck, r_const], W=[r_nckg])
            op(DVE, lambda e: e.tensor_scalar(negck_own[:, :, :, :], nck_x[:, :, 0, :, :], ccore[:, 0:1], None, ALU.mult),
               R=[r_nck, r_const], W=[r_ncko])
            for jp in range(1, 4):
                op(DVE, lambda e: e.scalar_tensor_tensor(negck_own[:, :, :, :], nck_x[:, :, jp, :, :], ccore[:, jp:jp + 1],
                                                         negck_own[:, :, :, :], ALU.mult, ALU.add),
                   R=[r_nck, r_ncko, r_const], W=[r_ncko])
            for E in (QS, DVE, ACT):
                wait_alias(E, [r_lfc, r_chm])

            kvcnt = 0
            srr = 0
            tcnt = 0
            pcnt = 0
            hcnt = 0
            for i in range(NSLOT):
                c0m = SW * i + HALO
                for c in range(4):
                    dma(QW, dq["wm"], wqo[:, 2 * c:2 * c + 2, :],
                        w_q[:, 2 * c * D:(2 * c + 2) * D].rearrange("p (c d) -> p c d", c=2), W=[r_wqo])
                for h in range(NH):
                    b = srr % 4
                    srr += 1
                    for dc in range(DC):
                        op(PE, lambda e: e.matmul(bank[b][:, :], wqo[:, dc, h * 128:(h + 1) * 128], hT[:, dc, c0m:c0m + SLOT],
                                                  start=(dc == 0), stop=(dc == DC - 1)),
                           R=[r_wqo, r_hm[i]], W=[bres[b]], inc=(dc == DC - 1))
                    op(ACT, lambda e: e.activation(qT_all[:, h, :], bank[b][:, :], AF.Identity, bias=0.0, scale=float(QSCALE)),
                       R=[bres[b]], W=[r_qT])
                    flush()
                for c in range(4):
                    dma(QW, dq["wm"], wqo[:, 2 * c:2 * c + 2, :],
                        w_o[:, 2 * c * D:(2 * c + 2) * D].rearrange("p (c d) -> p c d", c=2), W=[r_wqo])
                hctx = []
                blks = []
                tiles_all = []
                for h in range(NH):
                    ob = 4 + 2 * (hcnt % 2)
                    hcnt += 1
                    tiles = [("meta", 0, 0)]
                    for ip in range(i):
                        for jp in range(4):
                            tiles.append(("past", ip, jp))
                    for jp in range(3):
                        tiles.append(("grp", i, jp))
                    tiles.append(("own", i, 0))
                    first_b = len(blks)
                    for (kind, ip, jp) in tiles:
                        ti = len(tiles_all)
                        tiles_all.append((h, kind, ip, jp, kvcnt % NKV))
                        kvcnt += 1
                        for bb in ([0] if kind == "meta" else range(4)):
                            blks.append((ti, h, kind, ip, jp, bb, HALO if kind == "meta" else 128))
                    hctx.append(dict(ob=ob, sb=ob + 1, cq=cqb[h % 2], r_cq=r_cqb[h % 2], first=first_b, last=len(blks) - 1))

                def emit_loads(ti):
                    h, kind, ip, jp, s_ = tiles_all[ti]
                    dsem = dq["kv%d" % s_]
                    ksem = dq["kk%d" % s_]
                    if kind == "meta":
                        dma(QS, ksem, kring[s_][:, 0:HALO], kTS[h * 128:(h + 1) * 128, 0:HALO], R=[r_kvs], W=[r_kr[s_]])
                        dma(QS, dsem, vring[s_][0:HALO, 0, :], vS[h * 128:h * 128 + HALO, 16 * 128:17 * 128],
                            R=[r_kvs], W=[r_vr[s_]])
                    elif kind == "own":
                        dma(QS, ksem, kring[s_][:, :], kT_loc[h * 128:(h + 1) * 128, HALO + SLOT * i:HALO + SLOT * (i + 1)],
                            W=[r_kr[s_]])
                        dma(QS, dsem, vring[s_].rearrange("p b e -> p (b e)"),
                            v_loc[h * 128:(h + 1) * 128, 4 * i * 128:(4 * i + 4) * 128], W=[r_vr[s_]])
                    else:
                        r0 = jp * D + h * 128
                        dma(QS, ksem, kring[s_][:, :], kTS[r0:r0 + 128, HALO + SLOT * ip:HALO + SLOT * (ip + 1)],
                            R=[r_kvs], W=[r_kr[s_]])
                        dma(QS, dsem, vring[s_].rearrange("p b e -> p (b e)"),
                            vS[r0:r0 + 128, 4 * ip * 128:(4 * ip + 4) * 128], R=[r_kvs], W=[r_vr[s_]])

                for ti in range(min(NKV, len(tiles_all))):
                    emit_loads(ti)
                LOOK = 3
                state = {}
                last_blk_of_tile = {}
                for bi_, bl in enumerate(blks):
                    last_blk_of_tile[bl[0]] = bi_

                def head_setup(h):
                    nonlocal srr
                    cq, r_cq = hctx[h]["cq"], hctx[h]["r_cq"]
                    cb = srr % 4
                    srr += 1
                    op(DVE, lambda e: e.tensor_scalar(cqm[0:8, :], cq_own[0:8, i, :], ident[0:8, h:h + 1], None, ALU.mult),
                       R=[r_cqo, r_const], W=[r_cqm])
                    op(PE, lambda e: e.matmul(bank[cb][:, :], sel8[0:8, :], cqm[0:8, :], start=True, stop=True),
                       R=[r_cqm, r_const], W=[bres[cb]])
                    op(DVE, lambda e: e.tensor_copy(cq[:, :], bank[cb][:, :]), R=[bres[cb]], W=[r_cq])

                def head_finish(h):
                    ob, sb_ = hctx[h]["ob"], hctx[h]["sb"]
                    op(DVE, lambda e: e.reciprocal(rsum[:, :], bank[sb_][:, :]), R=[bres[sb_]], W=[r_rsum])
                    op(DVE, lambda e: e.tensor_tensor(aoT[:, h, :], bank[ob][:, :], rsum[:, :], ALU.mult),
                       R=[bres[ob], r_rsum], W=[r_aoT])

                def front(bi_):
                    nonlocal srr, tcnt, pcnt
                    ti, h, kind, ip, jp, bb, nk = blks[bi_]
                    if bi_ == hctx[h]["first"]:
                        head_setup(h)
                    cq, r_cq = hctx[h]["cq"], hctx[h]["r_cq"]
                    s_ = tiles_all[ti][4]
                    qlo = 128 * bb if kind == "own" else 0
                    sbk = srr % 4
                    srr += 1
                    tb = tbuf[tcnt % 3]
                    r_t = r_tb[tcnt % 3]
                    tcnt += 1
                    pb_ = pbuf[pcnt % 5]
                    r_p = r_pb[pcnt % 5]
                    pcnt += 1
                    op(PE, lambda e: e.matmul(bank[sbk][0:nk, qlo:SLOT], kring[s_][:, bb * 128:bb * 128 + nk],
                                              qT_all[:, h, qlo:SLOT], start=True, stop=True),
                       R=[r_kr[s_], r_qT], W=[bres[sbk]])
                    op(DVE, lambda e: e.tensor_tensor(tb[0:nk, qlo:SLOT], bank[sbk][0:nk, qlo:SLOT], cq[0:nk, qlo:SLOT],
                                                      ALU.add), R=[bres[sbk], r_cq], W=[r_t])
                    if kind == "own":
                        op(DVE, lambda e: e.tensor_tensor(tb[:, qlo:qlo + 128], tb[:, qlo:qlo + 128], tri[:, :], ALU.add),
                           R=[r_t, r_const], W=[r_t])
                    if kind == "meta":
                        bias_ap, r_b = negck[0:HALO, 0, h:h + 1], r_nck
                    elif kind == "past":
                        bias_ap, r_b = negck[:, 1 + 16 * ip + 4 * jp + bb, h:h + 1], r_nck
                    elif kind == "grp":
                        bias_ap, r_b = negck_grp[:, i, jp, bb, h:h + 1], r_nckg
                    else:
                        bias_ap, r_b = negck_own[:, i, bb, h:h + 1], r_ncko
                    op(ACT, lambda e: e.activation(pb_[0:nk, qlo:SLOT], tb[0:nk, qlo:SLOT], AF.Exp, bias=bias_ap, scale=1.0),
                       R=[r_t, r_b], W=[r_p])
                    state[bi_] = (pb_, r_p, qlo)

                def back(bi_):
                    ti, h, kind, ip, jp, bb, nk = blks[bi_]
                    s_ = tiles_all[ti][4]
                    ob, sb_ = hctx[h]["ob"], hctx[h]["sb"]
                    pb_, r_p, qlo = state.pop(bi_)
                    first = bi_ == hctx[h]["first"]
                    last = bi_ == hctx[h]["last"]
                    op(PE, lambda e: e.matmul(bank[ob][:, qlo:SLOT], vring[s_][0:nk, bb, :], pb_[0:nk, qlo:SLOT],
                                              start=first, stop=last),
                       R=[r_vr[s_], r_p], W=[bres[ob]], inc=False)
                    op(PE, lambda e: e.matmul(bank[sb_][:, qlo:SLOT], ones16[0:nk, :], pb_[0:nk, qlo:SLOT],
                                              start=first, stop=last),
                       R=[r_p, r_const], W=[bres[sb_]], inc=True)
                    if last:
                        head_finish(h)
                    if last_blk_of_tile[ti] == bi_ and ti + NKV < len(tiles_all):
                        emit_loads(ti + NKV)

                for idx in range(len(blks) + LOOK):
                    if idx < len(blks):
                        front(idx)
                    if idx - LOOK >= 0:
                        back(idx - LOOK)
                for q in range(4):
                    m = 4 * i + q
                    pb2 = 4 + (q % 2) * 2
                    for half in range(2):
                        for h in range(NH):
                            op(PE, lambda e: e.matmul(bank[pb2 + half][:, :], aoT[:, h, 128 * q:128 * (q + 1)],
                                                      wqo[:, h, half * 512:half * 512 + 512], start=(h == 0), stop=(h == NH - 1)),
                               R=[r_aoT, r_wqo], W=[bres[pb2 + half]], inc=(h == NH - 1))
                    flush()
                    layer_norm(pb2, pb2 + 1, K_MIX, 4, resid[:, m, :], r_resid[m], 128, SW * i + HALO + 128 * q, r_hm[i])
            flush()
            alias_h = [r_cqm, r_wqo, r_aoT, r_qT, r_rsum, r_cqo, r_nck, r_nckg, r_ncko] + r_kr + r_vr + r_tb + r_pb + r_cqb
            for E in (QW, DVE, ACT):
                wait_alias(E, alias_h)

            ffn(1, 1, 5, "none", final=True)
            for m in range(16):
                dma(QS, dq["out"], out_d[m * 128:(m + 1) * 128, :], resid[:, m, :], R=[r_resid[m]])
            QS.wait(dq["out"], dq["out"].n)

        for sname, sm in [(x.name, x) for x in (s_pe, s_dve, s_act, s_pool, s_cc)] + [(k, v) for k, v in dq.items()]:
            assert sm.n < 60000, (sname, sm.n)

        def sync_special(e, what):
            breg = es.enter_context(e.register("breg"))
            e.reg_load(breg, bidx[0:1, 0:1])
            dynctx["bval"] = e.snap(breg, min_val=0, max_val=1)

        @block.tensor
        def _(e):
            PE.e.replay(e)

        @block.vector
        def _(e):
            DVE.e.replay(e)

        @block.scalar
        def _(e):
            ACT.e.replay(e)

        @block.gpsimd
        def _(e):
            QW.e.replay(e, sync_special)

        @block.sync
        def _(e):
            QS.e.replay(e, sync_special if do_l1 else None)
    return nc


def _bf16_dtype():
    import ml_dtypes
    return ml_dtypes.bfloat16


def _host_consts(ln_gain, ln_bias, conv_w, f_bias):
    c = {}
    c["c_ident"] = np.eye(128, dtype=np.float32)
    c["c_ones16"] = np.ones((128, 128), dtype=_bf16_dtype())
    k = np.arange(128)[:, None]
    q = np.arange(128)[None, :]
    c["c_tri"] = np.where(k <= q, 0.0, NEG).astype(np.float32)
    c["c_sel8"] = np.ones((8, 128), dtype=np.float32)
    g = np.asarray(ln_gain, dtype=np.float32).reshape(6, DC, 128)
    b = np.asarray(ln_bias, dtype=np.float32).reshape(6, DC, 128)
    c["c_gcol"] = np.ascontiguousarray(g.transpose(2, 0, 1).reshape(128, 6 * DC))
    c["c_bcol"] = np.ascontiguousarray(b.transpose(2, 0, 1).reshape(128, 6 * DC))
    gb = np.stack([np.asarray(ln_gain, np.float32).reshape(6, D), np.asarray(ln_bias, np.float32).reshape(6, D)], axis=1)
    c["c_gbc"] = np.ascontiguousarray(np.broadcast_to(gb[:, :, None, :], (6, 2, 128, D)))
    cwv = np.asarray(conv_w, np.float32)[0].reshape(3, DC, 128)
    c["c_cw"] = np.ascontiguousarray(cwv.transpose(2, 1, 0).reshape(128, DC * 3))
    c["c_fb"] = np.asarray(f_bias, np.float32).reshape(8, 1)
    c["c_onescol"] = np.ones((128, 4), dtype=np.float32)
    return c


def _core_consts(cidx):
    b, j = divmod(cidx, 4)
    cc = np.zeros((128, 8), dtype=np.float32)
    cc[:, j] = 1.0
    for jp in range(4):
        cc[:, 4 + jp] = 0.0 if jp < j else NEG
    return {"c_core": cc, "c_bidx": np.array([[b, 0, 0, 0]], dtype=np.int32)}


def _w1_layout(wg, wu):
    g = np.asarray(wg, np.float32).reshape(DC, 128, FC, 128).transpose(2, 1, 0, 3)
    u = np.asarray(wu, np.float32).reshape(DC, 128, FC, 128).transpose(2, 1, 0, 3)
    return np.ascontiguousarray(np.stack([g, u], axis=2).reshape(FC, 128, 2 * D))


def _rows_layout(w, ncols):
    return np.ascontiguousarray(np.asarray(w, np.float32).reshape(DC, 128, ncols).transpose(1, 0, 2).reshape(128, DC * ncols))


def _xin(x, meta, cidx):
    b, j = divmod(cidx, 4)
    parts = []
    for i in range(NSLOT):
        s0 = SLOT * (4 * i + j)
        parts.append(meta if s0 == 0 else x[b, s0 - HALO:s0])
        parts.append(x[b, s0:s0 + SLOT])
    return np.ascontiguousarray(np.concatenate(parts, axis=0).astype(np.float32))


def _weights(inputs, l0, l1):
    w = {}
    for l in range(2):
        if (l == 0 and l0) or (l == 1 and l1):
            w[f"w1_{l}0"] = _w1_layout(inputs["ffn1_wg"][l], inputs["ffn1_wu"][l])
            w[f"w1_{l}1"] = _w1_layout(inputs["ffn2_wg"][l], inputs["ffn2_wu"][l])
            w[f"wd_{l}0"] = np.ascontiguousarray(np.asarray(inputs["ffn1_wd"][l], np.float32))
            w[f"wd_{l}1"] = np.ascontiguousarray(np.asarray(inputs["ffn2_wd"][l], np.float32))
    if l0:
        cin = np.asarray(inputs["conv_w_in"][0], np.float32).reshape(DC, 128, 3, DC, 128)
        w["w_cin"] = np.ascontiguousarray(cin.transpose(3, 1, 0, 2, 4).reshape(DC, 128, 3 * D))
        w["w_cout"] = _rows_layout(inputs["conv_w_out"][0], D)
        w["w_kv"] = _rows_layout(inputs["kv_w"], 2056)
    if l1:
        w["w_q"] = _rows_layout(inputs["attn_w_q"][0], D)
        w["w_o"] = _rows_layout(inputs["attn_w_o"][0], D)
    return w


MODE = "fused"


def kernel(**inputs):
    x = np.asarray(inputs["x"], np.float32)
    meta = np.asarray(inputs["meta"], np.float32)
    consts = _host_consts(inputs["ln_gain"], inputs["ln_bias"], inputs["conv_w"], inputs["f_bias"])
    cores = list(range(8))
    if MODE == "fused":
        wts = _weights(inputs, True, True)
        maps = []
        for c in cores:
            m = dict(consts)
            m.update(_core_consts(c))
            m.update(wts)
            m["xin"] = _xin(x, meta, c)
            maps.append(m)
        res = run_bass_kernel_spmd(build_program("fused"), maps, core_ids=cores)
        outs = [r["out"] for r in res.results]
    else:
        wts0 = _weights(inputs, True, False)
        maps = []
        for c in cores:
            m = dict(consts)
            m.update(_core_consts(c))
            m.update(wts0)
            m["xin"] = _xin(x, meta, c)
            maps.append(m)
        r0 = run_bass_kernel_spmd(build_program("l0"), maps, core_ids=cores).results
        kTG = np.ascontiguousarray(np.concatenate([r["kT_loc"] for r in r0], axis=0))
        vG = np.ascontiguousarray(np.concatenate([r["v_loc"] for r in r0], axis=0))
        lfG = np.ascontiguousarray(np.concatenate([r["lf_loc"] for r in r0], axis=0))
        wts1 = _weights(inputs, False, True)
        maps = []
        for c in cores:
            m = dict(consts)
            m.update(_core_consts(c))
            m.update(wts1)
            for kname in ("resid_x", "hT_x", "kT_loc", "v_loc"):
                m[kname] = r0[c][kname]
            m["kTG"], m["vG"], m["lfG"] = kTG, vG, lfG
            maps.append(m)
        r1 = run_bass_kernel_spmd(build_program("l1"), maps, core_ids=cores).results
        outs = [r["out"] for r in r1]
    out = np.empty((2, SEQ, D), dtype=np.float32)
    for c in cores:
        b, j = divmod(c, 4)
        for i in range(NSLOT):
            s0 = SLOT * (4 * i + j)
            out[b, s0:s0 + SLOT] = outs[c][SLOT * i:SLOT * (i + 1)]
    return out
```
